# Optimizing a Trainium2 kernel written in Bass

```python
import jax
import jax.numpy as jnp
from jax import lax
import numpy as np

D_MODEL = 1024
BATCH = 2
SEQ = 16384
DEPTH = 4

CHUNK = 64
HGRN_BLOCK = 16
SB_BLOCK = 128
N_EVEN = (DEPTH + 1) // 2
N_ODD = DEPTH // 2
A_HEADS = 4
A_DK = 128
A_DV = 128
A_QK = A_HEADS * A_DK
A_V = A_HEADS * A_DV
B_HEADS = 4
B_HD = 128
B_WIDTH = B_HEADS * B_HD
P_EVEN = 2 * A_QK + 2 * A_V + 3 * B_WIDTH
MIX_EVEN = A_V + B_WIDTH
C_HEADS = 8
C_DQK = 64
C_DV = 128
C_QK = C_HEADS * C_DQK
C_V = C_HEADS * C_DV
C_CONV = 4
P_ODD = 2 * C_QK + 2 * C_V + 2 * C_HEADS
FFN_DIM = 2816
FFN_CONV = 3
EPS = 1e-6

kernel_name = 'hybrid_hgrn2_stickbreak_mlstm_convglu'


def rmsnorm(x, g):
    xf = x.astype(jnp.float32)
    y = xf * lax.rsqrt(jnp.mean(xf * xf, axis=-1, keepdims=True) + EPS)
    return y * g.astype(jnp.float32)


def modulate(x, g, shift, scale):
    return (rmsnorm(x, g) * (1.0 + scale) + shift).astype(x.dtype)


def head_rmsnorm(o, g):
    b, t, h, d = o.shape
    y = o * lax.rsqrt(jnp.mean(o * o, axis=-1, keepdims=True) + EPS)
    return y.reshape(b, t, h * d) * g.astype(jnp.float32)


def causal_dwconv(x, w, b):
    k = w.shape[0]
    y = lax.conv_general_dilated(x, w[:, None, :].astype(x.dtype), window_strides=(1,),
                                 padding=[(k - 1, 0)], dimension_numbers=('NWC', 'WIO', 'NWC'),
                                 feature_group_count=x.shape[-1])
    return y + b.astype(x.dtype)


def to_chunks(a, size):
    bsz, t = a.shape[0], a.shape[1]
    a = a.reshape((bsz, t // size, size) + a.shape[2:])
    return jnp.swapaxes(jnp.moveaxis(a, 1, 0), 2, 3)


def from_chunks(o):
    n, bsz, h, l, d = o.shape
    return jnp.moveaxis(jnp.swapaxes(o, 2, 3), 0, 1).reshape(bsz, n * l, h, d)


def hgrn2_scan(q, log_f, k, v):
    bsz = q.shape[0]
    mask = jnp.tril(jnp.ones((HGRN_BLOCK, HGRN_BLOCK), dtype=bool))

    def step(state, inp):
        qc, lf, kc, vc = inp
        b = jnp.cumsum(lf, axis=2)
        rel = jnp.where(mask[:, :, None], b[:, :, :, None, :] - b[:, :, None, :, :], -jnp.inf)
        scores = jnp.einsum('bhtd,bhsd,bhtsd->bhts', qc, kc, jnp.exp(rel))
        o = (jnp.einsum('bhts,bhse->bhte', scores, vc)
             + jnp.einsum('bhtd,bhde->bhte', qc * jnp.exp(b), state))
        b_end = b[:, :, -1:, :]
        state = (jnp.exp(b_end[:, :, 0, :])[..., None] * state
                 + jnp.einsum('bhsd,bhse->bhde', kc * jnp.exp(b_end - b), vc))
        return state, o

    s0 = jnp.zeros((bsz, A_HEADS, A_DK, A_DV), jnp.float32)
    xs = (to_chunks(q, HGRN_BLOCK), to_chunks(log_f, HGRN_BLOCK),
          to_chunks(k, HGRN_BLOCK), to_chunks(v, HGRN_BLOCK))
    _, o = lax.scan(step, s0, xs)
    return from_chunks(o)


def stick_breaking(q, k, v):
    bsz, t, h, hd = q.shape
    nb = t // SB_BLOCK

    def blocks(a):
        return a.reshape(bsz, nb, SB_BLOCK, h, hd).transpose(0, 3, 1, 2, 4)

    qb = blocks(q * (hd ** -0.5))
    kb = blocks(k)
    vb = blocks(v)
    incl = jnp.tril(jnp.ones((SB_BLOCK, SB_BLOCK), jnp.float32))
    strict = jnp.tril(jnp.ones((SB_BLOCK, SB_BLOCK), dtype=bool), k=-1)
    out = jnp.zeros((bsz, h, nb, SB_BLOCK, hd), jnp.float32)
    acc = jnp.zeros((bsz, h, nb, SB_BLOCK), jnp.float32)
    for d in range(nb):
        z = jnp.einsum('bhnqd,bhnkd->bhnqk', qb[:, :, d:], kb[:, :, :nb - d]).astype(jnp.float32)
        lk = jax.nn.log_sigmoid(-z)
        if d == 0:
            lk = jnp.where(strict, lk, 0.0)
        r = jnp.einsum('bhnqj,js->bhnqs', lk, incl) + acc[:, :, d:, :, None]
        a = jnp.exp(z + r)
        if d == 0:
            a = jnp.where(strict, a, 0.0)
        out = out.at[:, :, d:].add(
            jnp.einsum('bhnqk,bhnkd->bhnqd', a, vb[:, :, :nb - d].astype(jnp.float32)))
        acc = acc.at[:, :, d:].set(r[..., 0])
    return out.transpose(0, 2, 3, 1, 4).reshape(bsz, t, h, hd)


def mlstm_scan(q, k, v, ig, lf):
    bsz = q.shape[0]
    mask = jnp.tril(jnp.ones((CHUNK, CHUNK), dtype=bool))

    def step(carry, inp):
        cmem, nvec, m = carry
        qc, kc, vc, igc, lfc = inp
        b = jnp.cumsum(lfc, axis=-1)
        log_d = jnp.where(mask, b[..., :, None] - b[..., None, :] + igc[..., None, :], -jnp.inf)
        log_inter = b + m[..., None]
        m_t = jnp.maximum(log_inter, jnp.max(log_d, axis=-1))
        w = jnp.einsum('bhtd,bhsd->bhts', qc, kc) * jnp.exp(log_d - m_t[..., None])
        w_inter = jnp.exp(log_inter - m_t)
        num = (jnp.einsum('bhts,bhse->bhte', w, vc)
               + w_inter[..., None] * jnp.einsum('bhtd,bhde->bhte', qc, cmem))
        den = jnp.sum(w, axis=-1) + w_inter * jnp.einsum('bhtd,bhd->bht', qc, nvec)
        hc = num / jnp.maximum(jnp.abs(den), jnp.exp(-m_t))[..., None]
        m_new = m_t[..., -1]
        w_end = jnp.exp(b[..., -1:] - b + igc - m_new[..., None])
        decay = jnp.exp(b[..., -1] + m - m_new)
        cmem = decay[..., None, None] * cmem + jnp.einsum('bhs,bhsd,bhse->bhde', w_end, kc, vc)
        nvec = decay[..., None] * nvec + jnp.einsum('bhs,bhsd->bhd', w_end, kc)
        return (cmem, nvec, m_new), hc

    init = (jnp.zeros((bsz, C_HEADS, C_DQK, C_DV), jnp.float32),
            jnp.zeros((bsz, C_HEADS, C_DQK), jnp.float32),
            jnp.zeros((bsz, C_HEADS), jnp.float32))
    xs = (to_chunks(q, CHUNK), to_chunks(k, CHUNK), to_chunks(v, CHUNK),
          to_chunks(ig, CHUNK), to_chunks(lf, CHUNK))
    _, hc = lax.scan(step, init, xs)
    return from_chunks(hc)


def even_mixer(h, w_in, lb, norm_g, w_out):
    bsz, t, _ = h.shape
    p = jnp.matmul(h, w_in).astype(jnp.float32)
    splits = [A_QK, 2 * A_QK, 2 * A_QK + A_V, 2 * A_QK + 2 * A_V,
              2 * A_QK + 2 * A_V + B_WIDTH, 2 * A_QK + 2 * A_V + 2 * B_WIDTH]
    qa, fa, ia, ga, qb, kb, vb = jnp.split(p, splits, axis=-1)
    lb_h = lb.astype(jnp.float32).reshape(A_HEADS, A_DK)
    zf = fa.reshape(bsz, t, A_HEADS, A_DK)
    log_f = jnp.logaddexp(jnp.log(lb_h), jnp.log1p(-lb_h) + jax.nn.log_sigmoid(zf))
    k_a = (1.0 - lb_h) * jax.nn.sigmoid(-zf)
    o_a = hgrn2_scan(qa.reshape(bsz, t, A_HEADS, A_DK), log_f, k_a, ia.reshape(bsz, t, A_HEADS, A_DV))
    o_a = head_rmsnorm(o_a, norm_g) * jax.nn.silu(ga)
    o_b = stick_breaking(qb.reshape(bsz, t, B_HEADS, B_HD), kb.reshape(bsz, t, B_HEADS, B_HD),
                         vb.reshape(bsz, t, B_HEADS, B_HD)).reshape(bsz, t, B_WIDTH)
    o = jnp.concatenate([o_a, o_b], axis=-1).astype(h.dtype)
    return jnp.matmul(o, w_out)


def odd_mixer(h, w_in, conv_w, conv_b, gate_b, norm_g, w_out):
    bsz, t, _ = h.shape
    p = jnp.matmul(h, w_in).astype(jnp.float32)
    qk, v, og, gates = jnp.split(p, [2 * C_QK, 2 * C_QK + C_V, 2 * C_QK + 2 * C_V], axis=-1)
    qk = jax.nn.silu(causal_dwconv(qk, conv_w.astype(jnp.float32), conv_b.astype(jnp.float32)))
    q, k = jnp.split(qk, 2, axis=-1)
    q = q.reshape(bsz, t, C_HEADS, C_DQK)
    k = k.reshape(bsz, t, C_HEADS, C_DQK) * (C_DQK ** -0.5)
    gates = gates + gate_b.astype(jnp.float32)
    ig = gates[..., :C_HEADS]
    lf = jax.nn.log_sigmoid(gates[..., C_HEADS:])
    hc = mlstm_scan(q, k, v.reshape(bsz, t, C_HEADS, C_DV), ig, lf)
    o = head_rmsnorm(hc, norm_g) * jax.nn.sigmoid(og)
    return jnp.matmul(o.astype(h.dtype), w_out)


def conv_glu_ffn(h, w_in, conv_w, conv_b, w_out):
    u = jnp.matmul(h, w_in)
    a, g = jnp.split(u, 2, axis=-1)
    a = causal_dwconv(a, conv_w, conv_b)
    return jnp.matmul(jax.nn.gelu(a) * g, w_out)


def setup_inputs(seed: int = 0) -> dict:
    key = jax.random.key(seed)
    ks = jax.random.split(key, 24)

    def nrm(k, shape, scale):
        return jax.random.normal(k, shape, jnp.float32) * scale

    d = D_MODEL
    gate_b = jnp.concatenate([
        nrm(ks[13], (N_ODD, C_HEADS), 0.1),
        jnp.linspace(3.0, 6.0, C_HEADS, dtype=jnp.float32)[None, :] + nrm(ks[14], (N_ODD, C_HEADS), 0.1)], axis=-1)
    return {
        'x': nrm(ks[0], (BATCH, SEQ, d), 1.0),
        'c': nrm(ks[1], (BATCH, d), 1.0),
        'ada_w': nrm(ks[2], (DEPTH, d, 6 * d), 0.5 * d ** -0.5),
        'ada_b': nrm(ks[3], (DEPTH, 6 * d), 0.02),
        'norm_mix_g': 1.0 + nrm(ks[4], (DEPTH, d), 0.02),
        'norm_ffn_g': 1.0 + nrm(ks[5], (DEPTH, d), 0.02),
        'even_w_in': nrm(ks[6], (N_EVEN, d, P_EVEN), d ** -0.5),
        'even_w_out': nrm(ks[7], (N_EVEN, MIX_EVEN, d), MIX_EVEN ** -0.5),
        'hgrn_lb': nrm(ks[8], (N_EVEN, A_QK), 0.5),
        'hgrn_norm_g': 1.0 + nrm(ks[9], (N_EVEN, A_V), 0.02),
        'odd_w_in': nrm(ks[10], (N_ODD, d, P_ODD), d ** -0.5),
        'odd_conv_w': nrm(ks[11], (N_ODD, C_CONV, 2 * C_QK), C_CONV ** -0.5),
        'odd_conv_b': nrm(ks[12], (N_ODD, 2 * C_QK), 0.02),
        'odd_gate_b': gate_b,
        'odd_norm_g': 1.0 + nrm(ks[15], (N_ODD, C_V), 0.02),
        'odd_w_out': nrm(ks[16], (N_ODD, C_V, d), C_V ** -0.5),
        'ffn_w_in': nrm(ks[17], (DEPTH, d, 2 * FFN_DIM), d ** -0.5),
        'ffn_conv_w': nrm(ks[18], (DEPTH, FFN_CONV, FFN_DIM), FFN_CONV ** -0.5),
        'ffn_conv_b': nrm(ks[19], (DEPTH, FFN_DIM), 0.02),
        'ffn_w_out': nrm(ks[20], (DEPTH, FFN_DIM, d), FFN_DIM ** -0.5),
        'final_g': 1.0 + nrm(ks[21], (d,), 0.02),
    }


def reference(x, c, ada_w, ada_b, norm_mix_g, norm_ffn_g, even_w_in, even_w_out, hgrn_lb,
              hgrn_norm_g, odd_w_in, odd_conv_w, odd_conv_b, odd_gate_b, odd_norm_g, odd_w_out,
              ffn_w_in, ffn_conv_w, ffn_conv_b, ffn_w_out, final_g):
    lb_all = jnp.cumsum(jax.nn.softmax(hgrn_lb.astype(jnp.float32), axis=0), axis=0)
    lb_all = lb_all - lb_all[:1]
    cs = jax.nn.silu(c)
    for layer in range(DEPTH):
        j = layer // 2
        mod = jnp.matmul(cs, ada_w[layer]) + ada_b[layer]
        sh_m, sc_m, g_m, sh_f, sc_f, g_f = [m[:, None, :] for m in jnp.split(mod, 6, axis=-1)]
        h = modulate(x, norm_mix_g[layer], sh_m, sc_m)
        if layer % 2 == 0:
            y = even_mixer(h, even_w_in[j], lb_all[j], hgrn_norm_g[j], even_w_out[j])
        else:
            y = odd_mixer(h, odd_w_in[j], odd_conv_w[j], odd_conv_b[j], odd_gate_b[j],
                          odd_norm_g[j], odd_w_out[j])
        x = x + (g_m * y).astype(x.dtype)
        h = modulate(x, norm_ffn_g[layer], sh_f, sc_f)
        x = x + (g_f * conv_glu_ffn(h, ffn_w_in[layer], ffn_conv_w[layer], ffn_conv_b[layer],
                                    ffn_w_out[layer])).astype(x.dtype)
    return rmsnorm(x, final_g).astype(x.dtype)
```

```python
import numpy as np
from contextlib import ExitStack
import concourse.bass as bass
import concourse.mybir as mybir
from concourse.bass_utils import run_bass_kernel_spmd

F32 = mybir.dt.float32
BF16 = mybir.dt.bfloat16
AF = mybir.ActivationFunctionType
ALU = mybir.AluOpType
AX = mybir.AxisListType

COMPUTE = ("tensor", "vector", "scalar", "gpsimd")
ENGINES = ("tensor", "vector", "scalar", "gpsimd", "sync")


class Tk:
    __slots__ = ("ap", "name", "w", "r", "dsem", "dcnt")

    def __init__(self, ap, name):
        self.ap = ap
        self.name = name
        self.w = None
        self.r = {}
        self.dsem = None
        self.dcnt = 0

    def __getitem__(self, idx):
        return self.ap[idx]


class Prog:
    def __init__(self, nc, stack):
        self.nc = nc
        self.stack = stack
        self.q = {e: [] for e in ENGINES}
        self.cnt = {e: 0 for e in COMPUTE}
        self.sem = {e: stack.enter_context(nc.semaphore("cs_" + e)) for e in COMPUTE}
        self.waited = {e: {} for e in ENGINES}
        self.nsem = len(COMPUTE)
        self.final = []
        self.uid = 0
        self.ninst = 0
        self.pending = {}

    def sb(self, shape, dtype=F32, name=None):
        self.uid += 1
        name = name or f"sb{self.uid}"
        t = self.stack.enter_context(self.nc.sbuf_tensor(name, list(shape), dtype))
        return Tk(t, name)

    def ps(self, shape, dtype=F32, name=None):
        self.uid += 1
        name = name or f"ps{self.uid}"
        t = self.stack.enter_context(self.nc.psum_tensor(name, list(shape), dtype))
        return Tk(t, name)

    def dram(self, name, shape, dtype=F32, kind="Internal"):
        t = self.nc.dram_tensor(name, list(shape), dtype, kind=kind)
        return Tk(t.ap(), name)

    def view(self, ap, name=None):
        self.uid += 1
        return Tk(ap, name or f"v{self.uid}")

    def newsem(self, name=None):
        self.uid += 1
        self.nsem += 1
        return self.stack.enter_context(self.nc.semaphore(name or f"ds{self.uid}"))

    def _deps(self, reads, writes):
        deps = []
        for t in reads:
            if t.w is not None:
                deps.append(t.w)
        for t in writes:
            if t.w is not None:
                deps.append(t.w)
            deps.extend(t.r.values())
        return deps

    def _filter(self, eng, deps):
        best = {}
        wd = self.waited[eng]
        for (sem, val) in deps:
            if eng == "tensor" and sem is self.sem["tensor"]:
                continue
            k = id(sem)
            if wd.get(k, 0) >= val:
                continue
            if k not in best or best[k][1] < val:
                best[k] = (sem, val)
        for k, (sem, val) in best.items():
            wd[k] = val
        return list(best.values())

    def _mark(self, reads, writes, tok):
        k = id(tok[0])
        for t in reads:
            t.r[k] = tok
        for t in writes:
            t.w = tok
            t.r = {}

    def barrier(self):
        for e in COMPUTE:
            deps = [(self.sem[f], self.cnt[f]) for f in COMPUTE if f != e and self.cnt[f] > 0]
            self.pending.setdefault(e, []).extend(deps)

    def emit(self, eng, fn, reads=(), writes=()):
        extra = self.pending.pop(eng, []) if hasattr(self, "pending") else []
        waits = self._filter(eng, self._deps(reads, writes) + extra)
        self.cnt[eng] += 1
        tok = (self.sem[eng], self.cnt[eng])
        self.q[eng].append((waits, fn, tok[0], 1))
        self._mark(reads, writes, tok)
        self.ninst += 1
        return tok

    def dma(self, out_ap, in_ap, reads=(), writes=(), sem_t=None, q="sync", final=False, **kw):
        if sem_t is None:
            sem_t = (list(writes) + list(reads))[0]
        if sem_t.dsem is None:
            sem_t.dsem = self.newsem()
        waits = self._filter(q, self._deps(reads, writes))
        sem_t.dcnt += 16
        tok = (sem_t.dsem, sem_t.dcnt)
        self.q[q].append((waits, lambda e: e.dma_start(out=out_ap, in_=in_ap, **kw), tok[0], 16))
        self._mark(reads, writes, tok)
        if final:
            self.final.append(tok)
        self.ninst += 1
        return tok

    def mm(self, out_t, out_ap, lhsT_t, lhsT_ap, rhs_t, rhs_ap, start=True, stop=True):
        rd = [lhsT_t, rhs_t]
        return self.emit("tensor", lambda e: e.matmul(out_ap, lhsT_ap, rhs_ap, start=start, stop=stop),
                         reads=rd, writes=[out_t])

    def tr(self, out_t, out_ap, in_t, in_ap, ident_t, ident_ap):
        return self.emit("tensor", lambda e: e.transpose(out_ap, in_ap, ident_ap),
                         reads=[in_t, ident_t], writes=[out_t])

    def act(self, out_t, out_ap, in_t, in_ap, func, reads=(), eng="scalar", **kw):
        return self.emit(eng, lambda e: e.activation(out=out_ap, in_=in_ap, func=func, **kw),
                         reads=[in_t] + list(reads), writes=[out_t])

    def finish(self):
        nc = self.nc
        fin = self._filter("sync", self.final)
        with nc.Block() as block:
            for e in ENGINES:
                lst = self.q[e]
                if not lst and not (e == "sync" and fin):
                    continue

                def body(engobj, lst=lst, e=e):
                    for (waits, fn, sem, inc) in lst:
                        for (s, v) in waits:
                            engobj.wait_ge(s, v)
                        fn(engobj).then_inc(sem, inc)
                    if e == "sync":
                        for (s, v) in fin:
                            engobj.wait_ge(s, v)

                getattr(block, e)(body)


D = 1024
FF = 2816
FC = 22
EPS = 1e-6


def _ident(P, dtype=F32, name="ident"):
    ident = P.sb([128, 128], dtype, name)
    P.emit("gpsimd", lambda e: e.memset(ident[:, :], 1.0), writes=[ident])
    P.emit("gpsimd", lambda e: e.affine_select(out=ident[:, :], in_=ident[:, :], pattern=[[1, 128]],
                                                compare_op=ALU.is_equal, fill=0.0, base=0,
                                                channel_multiplier=-1),
           reads=[ident], writes=[ident])
    return ident


class RR:
    def __init__(self, items):
        self.items = items
        self.i = 0

    def next(self):
        t = self.items[self.i % len(self.items)]
        self.i += 1
        return t


def build_B(NT, final):
    nc = bass.Bass("TRN2", target_bir_lowering=False)
    di = lambda n, s: nc.dram_tensor(n, s, F32, kind="ExternalInput").ap()
    xin = di("xin", [NT + 128, D])
    oin = di("oin", [NT + 128, D])
    wo_d = di("wo", [D, D])
    wi_d = di("wi", [D, 2 * FF])
    w2_d = di("w2", [FF, D])
    gm_d = di("gm_rep", [128, D])
    gf_d = di("gf_rep", [128, D])
    fg_d = di("fg_rep", [128, D])
    cols_d = di("cols", [128, 24])
    cw_d = di("cw", [128, 3 * FC])
    cb_d = di("cb", [128, FC])
    hm_d = di("hm", [128, 1])
    xout = nc.dram_tensor("xout", [NT, D], F32, kind="ExternalOutput").ap()
    with ExitStack() as st:
        P = Prog(nc, st)
        wo = P.sb([128, 8, D], BF16, "wo_sb")
        wi = P.sb([128, 8, 2 * FF], BF16, "wi_sb")
        w2 = P.sb([128, FC, D], BF16, "w2_sb")
        xt = [[P.sb([128, D], F32, f"xt{b}{s}") for s in range(2)] for b in range(2)]
        ot = [P.sb([128, D], F32, f"ot{s}") for s in range(2)]
        oT = P.sb([128, 8, 256], BF16, "oT")
        hT = P.sb([128, 8, 256], BF16, "hT")
        actT = P.sb([128, FC, 256], BF16, "actT")
        asb = RR([P.sb([128, 258], F32, f"asb{i}") for i in range(2)])
        t1s = RR([P.sb([128, 256], F32, f"t1_{i}") for i in range(2)])
        gls = RR([P.sb([128, 256], F32, f"gl_{i}") for i in range(2)])
        carry = P.sb([128, FC, 2], F32, "carry")
        cols = P.sb([128, 24], F32, "colsb")
        gs = P.sb([128, 8], F32, "gs")
        cw = P.sb([128, 3 * FC], F32, "cwsb")
        cb = P.sb([128, FC], F32, "cbsb")
        hm = P.sb([128, 1], F32, "hmsb")
        epsc = P.sb([128, 1], F32, "epsc")
        ss = RR([P.sb([128, 1], F32, f"ss{i}") for i in range(4)])
        rs = RR([P.sb([128, 1], F32, f"rs{i}") for i in range(4)])
        fg = P.sb([128, D], F32, "fgsb") if final else None
        ident = _ident(P)
        pT = RR([P.ps([128, 4, 128], F32, f"pT{i}") for i in range(2)])
        py = RR([P.ps([128, 512], F32, f"py{i}") for i in range(2)])
        pa = RR([P.ps([128, 512], F32, f"pa{i}") for i in range(2)])
        pg = RR([P.ps([128, 512], F32, f"pg{i}") for i in range(2)])

        P.emit("vector", lambda e: e.memset(epsc[:, :], EPS), writes=[epsc])
        P.dma(cols[:, :], cols_d, writes=[cols])
        P.dma(cw[:, :], cw_d, writes=[cw])
        P.dma(cb[:, :], cb_d, writes=[cb])
        P.dma(hm[:, :], hm_d, writes=[hm])
        if final:
            P.dma(fg[:, :], fg_d, writes=[fg])
        P.emit("vector", lambda e: e.scalar_tensor_tensor(out=gs[:, :], in0=cols[:, 8:16], scalar=1.0, in1=cols[:, 0:8],
                                                          op0=ALU.add, op1=ALU.mult), reads=[cols], writes=[gs])
        for k in range(8):
            P.dma(wi[:, k, :], wi_d[k * 128:(k + 1) * 128, :], writes=[wi], q="gpsimd")
        gmr, gfr = xt[0][0], xt[0][1]
        P.dma(gmr[:, :], gm_d, writes=[gmr])
        P.dma(gfr[:, :], gf_d, writes=[gfr])
        n = 0
        for k in range(8):
            stg = ot[n % 2]; n += 1
            P.dma(stg[:, :], wo_d[k * 128:(k + 1) * 128, :], writes=[stg])
            P.emit("vector", lambda e, stg=stg, k=k: e.tensor_tensor(out=wo[:, k, :], in0=stg[:, :], in1=gmr[:, :], op=ALU.mult),
                   reads=[stg, gmr], writes=[wo])
        for j in range(FC):
            stg = ot[n % 2]; n += 1
            P.dma(stg[:, :], w2_d[j * 128:(j + 1) * 128, :], writes=[stg])
            P.emit("vector", lambda e, stg=stg, j=j: e.tensor_tensor(out=w2[:, j, :], in0=stg[:, :], in1=gfr[:, :], op=ALU.mult),
                   reads=[stg, gfr], writes=[w2])

        def rstd(src, junk):
            s_, r_ = ss.next(), rs.next()
            P.emit("scalar", lambda e: e.activation(out=junk[:, :], in_=src[:, :], func=AF.Square, accum_out=s_[:, :]),
                   reads=[src], writes=[junk, s_])
            P.emit("scalar", lambda e: e.activation(out=r_[:, :], in_=s_[:, :], func=AF.Sqrt, scale=1.0 / D, bias=epsc[:, :]),
                   reads=[s_, epsc], writes=[r_])
            P.emit("vector", lambda e: e.reciprocal(out=r_[:, :], in_=r_[:, :]), reads=[r_], writes=[r_])
            return r_

        def do_tile(row0, orow0, ns, buf, halo_only):
            W = ns * 128
            X = xt[buf]
            for s in range(ns):
                P.dma(X[s][:, :], xin[row0 + s * 128: row0 + (s + 1) * 128, :], writes=[X[s]])
                P.dma(ot[s][:, :], oin[row0 + s * 128: row0 + (s + 1) * 128, :], writes=[ot[s]])
            for s in range(ns):
                for kk in range(2):
                    pt = pT.next()
                    for k4 in range(4):
                        k = kk * 4 + k4
                        P.tr(pt, pt[:, k4, :], ot[s], ot[s][:, k * 128:(k + 1) * 128], ident, ident[:, :])
                    P.emit("vector", lambda e, pt=pt, kk=kk, s=s: e.tensor_copy(
                        out=oT[:, kk * 4:(kk + 1) * 4, s * 128:(s + 1) * 128], in_=pt[:, :, :]),
                        reads=[pt], writes=[oT])
            for s in range(ns):
                for hf in range(2):
                    p_ = py.next()
                    for k in range(8):
                        P.mm(p_, p_[:, :], oT, oT[:, k, s * 128:(s + 1) * 128], wo, wo[:, k, hf * 512:(hf + 1) * 512],
                             start=(k == 0), stop=(k == 7))
                    P.emit("vector", lambda e, p_=p_, s=s, hf=hf: e.tensor_tensor(
                        out=X[s][:, hf * 512:(hf + 1) * 512], in0=p_[:, :], in1=X[s][:, hf * 512:(hf + 1) * 512], op=ALU.add),
                        reads=[p_, X[s]], writes=[X[s]])
                r_ = rstd(X[s], ot[s])
                P.emit("scalar", lambda e, s=s, r_=r_: e.activation(out=ot[s][:, :], in_=X[s][:, :], func=AF.Copy, scale=r_[:, :]),
                       reads=[X[s], r_], writes=[ot[s]])
            n_ev = 0
            for s in range(ns):
                for kk in range(2):
                    pt = pT.next()
                    for k4 in range(4):
                        k = kk * 4 + k4
                        P.tr(pt, pt[:, k4, :], ot[s], ot[s][:, k * 128:(k + 1) * 128], ident, ident[:, :])
                    for k4 in range(4):
                        k = kk * 4 + k4
                        if n_ev % 2 == 0:
                            P.emit("vector", lambda e, pt=pt, k=k, k4=k4, s=s: e.tensor_scalar(
                                out=hT[:, k, s * 128:(s + 1) * 128], in0=pt[:, k4, :], scalar1=gs[:, k:k + 1],
                                scalar2=cols[:, 16 + k:17 + k], op0=ALU.mult, op1=ALU.add),
                                reads=[pt, gs, cols], writes=[hT])
                        else:
                            P.emit("scalar", lambda e, pt=pt, k=k, k4=k4, s=s: e.activation(
                                out=hT[:, k, s * 128:(s + 1) * 128], in_=pt[:, k4, :], func=AF.Identity,
                                scale=gs[:, k:k + 1], bias=cols[:, 16 + k:17 + k]),
                                reads=[pt, gs, cols], writes=[hT])
                        n_ev += 1
            for j in range(FC):
                a_ = pa.next()
                for k in range(8):
                    P.mm(a_, a_[:, 0:W], wi, wi[:, k, j * 128:(j + 1) * 128], hT, hT[:, k, 0:W], start=(k == 0), stop=(k == 7))
                if halo_only:
                    P.emit("vector", lambda e, a_=a_, j=j: e.tensor_scalar(
                        out=carry[:, j, :], in0=a_[:, W - 2:W], scalar1=hm[:, 0:1], scalar2=None, op0=ALU.mult),
                        reads=[a_, hm], writes=[carry])
                    continue
                g_ = pg.next()
                for k in range(8):
                    P.mm(g_, g_[:, 0:W], wi, wi[:, k, FF + j * 128:FF + (j + 1) * 128], hT, hT[:, k, 0:W],
                         start=(k == 0), stop=(k == 7))
                ab = asb.next(); t1 = t1s.next(); gl = gls.next()
                P.emit("gpsimd", lambda e, ab=ab, j=j: e.tensor_copy(out=ab[:, 0:2], in_=carry[:, j, :]),
                       reads=[carry], writes=[ab])
                P.emit("scalar", lambda e, ab=ab, a_=a_: e.activation(out=ab[:, 2:2 + W], in_=a_[:, 0:W], func=AF.Copy),
                       reads=[a_], writes=[ab])
                P.emit("gpsimd", lambda e, ab=ab, j=j: e.tensor_copy(out=carry[:, j, :], in_=ab[:, W:W + 2]),
                       reads=[ab], writes=[carry])
                P.emit("vector", lambda e, ab=ab, t1=t1, j=j: e.tensor_scalar(
                    out=t1[:, 0:W], in0=ab[:, 0:W], scalar1=cw[:, j:j + 1], scalar2=cb[:, j:j + 1], op0=ALU.mult, op1=ALU.add),
                    reads=[ab, cw, cb], writes=[t1])
                for tap in (1, 2):
                    P.emit("vector", lambda e, ab=ab, t1=t1, j=j, tap=tap: e.scalar_tensor_tensor(
                        out=t1[:, 0:W], in0=ab[:, tap:tap + W], scalar=cw[:, tap * FC + j:tap * FC + j + 1], in1=t1[:, 0:W],
                        op0=ALU.mult, op1=ALU.add), reads=[ab, cw, t1], writes=[t1])
                P.emit("scalar", lambda e, t1=t1, gl=gl: e.activation(out=gl[:, 0:W], in_=t1[:, 0:W], func=AF.Gelu_apprx_tanh),
                       reads=[t1], writes=[gl])
                P.emit("vector", lambda e, gl=gl, g_=g_, j=j: e.tensor_tensor(
                    out=actT[:, j, 0:W], in0=gl[:, 0:W], in1=g_[:, 0:W], op=ALU.mult),
                    reads=[gl, g_], writes=[actT])
            if halo_only:
                return
            for s in range(ns):
                for hf in range(2):
                    p_ = py.next()
                    for j in range(FC):
                        P.mm(p_, p_[:, :], actT, actT[:, j, s * 128:(s + 1) * 128], w2, w2[:, j, hf * 512:(hf + 1) * 512],
                             start=(j == 0), stop=(j == FC - 1))
                    P.emit("vector", lambda e, p_=p_, s=s, hf=hf: e.tensor_tensor(
                        out=X[s][:, hf * 512:(hf + 1) * 512], in0=p_[:, :], in1=X[s][:, hf * 512:(hf + 1) * 512], op=ALU.add),
                        reads=[p_, X[s]], writes=[X[s]])
                if final:
                    r_ = rstd(X[s], ot[s])
                    P.emit("vector", lambda e, s=s, r_=r_: e.scalar_tensor_tensor(
                        out=X[s][:, :], in0=X[s][:, :], scalar=r_[:, 0:1], in1=fg[:, :], op0=ALU.mult, op1=ALU.mult),
                        reads=[X[s], r_, fg], writes=[X[s]])
                P.dma(xout[orow0 + s * 128: orow0 + (s + 1) * 128, :], X[s][:, :], reads=[X[s]], final=True)

        do_tile(0, 0, 1, 1, True)
        for i in range(NT // 256):
            do_tile(128 + i * 256, i * 256, 2, i % 2, False)
        P.finish()
    return nc


_NC_CACHE = {}


def _get_nc(key, builder):
    if key not in _NC_CACHE:
        _NC_CACHE[key] = builder()
    return _NC_CACHE[key]


def _col(v, n):
    return np.ascontiguousarray(np.asarray(v, np.float32).reshape(n, 128).T)


def _rep(v):
    return np.ascontiguousarray(np.broadcast_to(np.asarray(v, np.float32)[None, :], (128, v.shape[0])))


def run_B(x, o, layer, mod, inp, final):
    Bn, T, _ = x.shape
    ncore = 8
    cpb = ncore // Bn
    per = T // cpb
    nc = _get_nc(("B", per, final), lambda: build_B(per, final))
    j = layer // 2
    wo = inp["even_w_out"][j] if layer % 2 == 0 else inp["odd_w_out"][j]
    wi = inp["ffn_w_in"][layer]
    w2 = inp["ffn_w_out"][layer]
    cw = np.ascontiguousarray(inp["ffn_conv_w"][layer].reshape(3, FC, 128).transpose(2, 0, 1).reshape(128, 3 * FC))
    cb = _col(inp["ffn_conv_b"][layer], FC)
    fg_rep = _rep(inp["final_g"])
    in_maps = []
    for core in range(ncore):
        b = core // cpb
        c0 = (core % cpb) * per
        m = mod[b]
        g_m, sh_f, sc_f, g_f = m[2 * D:3 * D], m[3 * D:4 * D], m[4 * D:5 * D], m[5 * D:6 * D]
        if c0 == 0:
            prex = np.zeros((128, D), np.float32)
            preo = np.zeros((128, D), np.float32)
            hm = np.zeros((128, 1), np.float32)
        else:
            prex = x[b, c0 - 128:c0]
            preo = o[b, c0 - 128:c0]
            hm = np.ones((128, 1), np.float32)
        cols = np.concatenate([_col(inp["norm_ffn_g"][layer], 8), _col(sc_f, 8), _col(sh_f, 8)], axis=1)
        in_maps.append({
            "xin": np.ascontiguousarray(np.concatenate([prex, x[b, c0:c0 + per]], axis=0)),
            "oin": np.ascontiguousarray(np.concatenate([preo, o[b, c0:c0 + per]], axis=0)),
            "wo": wo, "wi": wi, "w2": w2,
            "gm_rep": _rep(g_m), "gf_rep": _rep(g_f), "fg_rep": fg_rep,
            "cols": np.ascontiguousarray(cols), "cw": cw, "cb": cb, "hm": hm,
        })
    res = run_bass_kernel_spmd(nc, in_maps, core_ids=list(range(ncore)))
    out = np.empty((Bn, T, D), np.float32)
    for core in range(ncore):
        b = core // cpb
        c0 = (core % cpb) * per
        out[b, c0:c0 + per] = res.results[core]["xout"]
    return out


class Pre:
    def __init__(self, P, cols_d):
        self.P = P
        self.cols = P.sb([128, 24], F32, "pre_cols")
        self.gs = P.sb([128, 8], F32, "pre_gs")
        self.epsc = P.sb([128, 1], F32, "pre_eps")
        self.ident = _ident(P, F32, "pre_ident")
        self.xt = RR([P.sb([128, D], F32, f"pre_x{i}") for i in range(2)])
        self.xn = RR([P.sb([128, D], F32, f"pre_xn{i}") for i in range(2)])
        self.hT = RR([P.sb([128, 8, 128], BF16, f"pre_hT{i}") for i in range(2)])
        self.ss = RR([P.sb([128, 1], F32, f"pre_ss{i}") for i in range(2)])
        self.rs = RR([P.sb([128, 1], F32, f"pre_rs{i}") for i in range(2)])
        cols, gs, epsc = self.cols, self.gs, self.epsc
        P.emit("vector", lambda e: e.memset(epsc[:, :], EPS), writes=[epsc])
        P.dma(cols[:, :], cols_d, writes=[cols])
        P.emit("vector", lambda e: e.scalar_tensor_tensor(out=gs[:, :], in0=cols[:, 8:16], scalar=1.0, in1=cols[:, 0:8],
                                                          op0=ALU.add, op1=ALU.mult), reads=[cols], writes=[gs])

    def run(self, x_rows_ap, pT):
        P = self.P
        X, XN, H = self.xt.next(), self.xn.next(), self.hT.next()
        s_, r_ = self.ss.next(), self.rs.next()
        epsc, gs, cols, ident = self.epsc, self.gs, self.cols, self.ident
        P.dma(X[:, :], x_rows_ap, writes=[X])
        P.emit("scalar", lambda e: e.activation(out=XN[:, :], in_=X[:, :], func=AF.Square, accum_out=s_[:, :]),
               reads=[X], writes=[XN, s_])
        P.emit("scalar", lambda e: e.activation(out=r_[:, :], in_=s_[:, :], func=AF.Sqrt, scale=1.0 / D, bias=epsc[:, :]),
               reads=[s_, epsc], writes=[r_])
        P.emit("vector", lambda e: e.reciprocal(out=r_[:, :], in_=r_[:, :]), reads=[r_], writes=[r_])
        P.emit("scalar", lambda e: e.activation(out=XN[:, :], in_=X[:, :], func=AF.Copy, scale=r_[:, :]),
               reads=[X, r_], writes=[XN])
        n_ev = 0
        for kk in range(2):
            pt = pT[kk]
            for k4 in range(4):
                k = kk * 4 + k4
                P.tr(pt, pt[:, k4, :], XN, XN[:, k * 128:(k + 1) * 128], ident, ident[:, :])
            for k4 in range(4):
                k = kk * 4 + k4
                if n_ev % 2 == 0:
                    P.emit("vector", lambda e, pt=pt, k=k, k4=k4: e.tensor_scalar(
                        out=H[:, k, :], in0=pt[:, k4, :], scalar1=gs[:, k:k + 1],
                        scalar2=cols[:, 16 + k:17 + k], op0=ALU.mult, op1=ALU.add),
                        reads=[pt, gs, cols], writes=[H])
                else:
                    P.emit("scalar", lambda e, pt=pt, k=k, k4=k4: e.activation(
                        out=H[:, k, :], in_=pt[:, k4, :], func=AF.Identity,
                        scale=gs[:, k:k + 1], bias=cols[:, 16 + k:17 + k]),
                        reads=[pt, gs, cols], writes=[H])
                n_ev += 1
        return H


def _tri_mask(P, name, dtype, val, base, cm, step, cmp=None):
    t = P.sb([128, 128], dtype, name)
    P.emit("gpsimd", lambda e: e.memset(t[:, :], val), writes=[t])
    P.emit("gpsimd", lambda e: e.affine_select(out=t[:, :], in_=t[:, :], pattern=[[step, 128]],
                                                compare_op=(cmp or ALU.is_ge), fill=0.0, base=base,
                                                channel_multiplier=cm), reads=[t], writes=[t])
    return t


NW_ODD = 772


def build_A_odd(T):
    nc = bass.Bass("TRN2", target_bir_lowering=False)
    di = lambda n, s: nc.dram_tensor(n, s, F32, kind="ExternalInput").ap()
    xin = di("xin", [T, D])
    cols_d = di("cols", [128, 24])
    w_d = di("w", [D, NW_ODD])
    cwqk_d = di("cwqk", [128, 8])
    cbqk_d = di("cbqk", [128, 2])
    gb_d = di("gb_rep", [128, 4])
    gn_d = di("gn_rep", [128, 256])
    oout = nc.dram_tensor("oout", [T, 256], F32, kind="ExternalOutput").ap()
    with ExitStack() as st:
        P = Prog(nc, st)
        pre = Pre(P, cols_d)
        W = P.sb([128, 8, NW_ODD], BF16, "W")
        for k in range(8):
            P.dma(W[:, k, :], w_d[k * 128:(k + 1) * 128, :], writes=[W], q="gpsimd")
        cwqk = P.sb([128, 8], F32, "cwqk_sb"); P.dma(cwqk[:, :], cwqk_d, writes=[cwqk])
        cbqk = P.sb([128, 2], F32, "cbqk_sb"); P.dma(cbqk[:, :], cbqk_d, writes=[cbqk])
        gb = P.sb([128, 4], F32, "gb_sb"); P.dma(gb[:, :], gb_d, writes=[gb])
        gn = P.sb([128, 256], F32, "gn_sb"); P.dma(gn[:, :], gn_d, writes=[gn])
        identb = _ident(P, BF16, "identb")
        mask8 = _tri_mask(P, "mask8", F32, 0.125, 0, -1, 1)
        triu = _tri_mask(P, "triu", F32, 1.0, 0, -1, 1)
        ones = P.sb([128, 128], F32, "ones")
        P.emit("gpsimd", lambda e: e.memset(ones[:, :], 1.0), writes=[ones])
        ext = [P.sb([128, 2, 131], F32, f"ext{i}") for i in range(2)]
        for t_ in ext:
            P.emit("gpsimd", lambda e, t_=t_: e.memset(t_[:, :, :], 0.0), writes=[t_])
        qc = P.sb([128, 2, 128], F32, "qc")
        qs = P.sb([128, 2, 128], BF16, "qs")
        gt = P.sb([128, 4], F32, "gt")
        e1 = P.sb([128, 2], F32, "e1")
        sp = P.sb([128, 2], F32, "sp")
        ws = P.sb([128, 2], F32, "ws")
        et = P.sb([128, 2], F32, "et")
        dect = P.sb([128, 2], F32, "dect")
        kd = P.sb([128, 2], F32, "kd")
        decc = P.sb([128, 1], F32, "decc")
        vaug = P.sb([128, 2, 129], BF16, "vaug")
        P.emit("gpsimd", lambda e: e.memset(vaug[:, :, :], 1.0), writes=[vaug])
        sig = P.sb([128, 256], F32, "sig")
        sT = RR([P.sb([128, 128], BF16, f"sT{i}") for i in range(2)])
        kdb = RR([P.sb([128, 128], BF16, f"kdb{i}") for i in range(2)])
        hmk = P.sb([128, 2], F32, "hmk")
        P.emit("gpsimd", lambda e: e.memset(hmk[:, :], 0.0), writes=[hmk])
        P.emit("gpsimd", lambda e: e.memset(hmk[0:64, 0:1], 1.0), reads=[hmk], writes=[hmk])
        P.emit("gpsimd", lambda e: e.memset(hmk[64:128, 1:2], 1.0), reads=[hmk], writes=[hmk])
        qm = P.sb([128, 2, 128], BF16, "qm")
        km = P.sb([128, 2, 128], BF16, "km")
        res = RR([P.sb([128, 129], F32, f"res{i}") for i in range(2)])
        junk = P.sb([128, 128], F32, "junk")
        sm = RR([P.sb([128, 8], F32, f"sm{i}") for i in range(2)])
        ytmp = P.sb([128, 128], F32, "ytmp")
        otile = RR([P.sb([128, 256], F32, f"otile{i}") for i in range(2)])
        Cf = P.sb([128, 129], F32, "Cf")
        Cb = P.sb([128, 129], BF16, "Cb")
        P.emit("gpsimd", lambda e: e.memset(Cf[:, :], 0.0), writes=[Cf])
        P.emit("gpsimd", lambda e: e.memset(Cb[:, :], 0.0), writes=[Cb])
        pT = [P.ps([128, 4, 128], F32, f"pT{i}") for i in range(2)]
        pA = P.ps([128, 512], F32, "pA")
        pqk = P.view(pA[:, 0:256].rearrange("p (i t) -> p i t", i=2))
        psc = P.view(pA[:, 256:384])
        pgt = P.view(pA[:, 384:392])
        pvo = P.ps([128, 512], F32, "pvo")
        pkt = P.ps([128, 128], BF16, "pkt")
        pCb = P.ps([128, 512], F32, "pCb")
        pB = P.ps([128, 512], F32, "pB")
        pnd = P.view(pB[:, 0:129])
        pC = P.view(pCb[:, 0:129])

        for c in range(T // 128):
            H = pre.run(xin[c * 128:(c + 1) * 128, :], pT)
            for i in range(2):
                for k in range(8):
                    P.mm(pqk, pqk[:, i, :], W, W[:, k, i * 128:(i + 1) * 128], H, H[:, k, :], start=(k == 0), stop=(k == 7))
            for k in range(8):
                P.mm(pvo, pvo[:, :], H, H[:, k, :], W, W[:, k, 256:768], start=(k == 0), stop=(k == 7))
            for k in range(8):
                P.mm(pgt, pgt[:, 0:4], H, H[:, k, :], W, W[:, k, 768:772], start=(k == 0), stop=(k == 7))
            E, Ep = ext[c % 2], ext[(c + 1) % 2]
            P.emit("scalar", lambda e, E=E: e.activation(out=E[:, :, 3:131], in_=pqk[:, :, :], func=AF.Copy),
                   reads=[pqk], writes=[E])
            P.emit("gpsimd", lambda e, E=E, Ep=Ep: e.tensor_copy(out=E[:, :, 0:3], in_=Ep[:, :, 128:131]),
                   reads=[Ep], writes=[E])
            for i in range(2):
                P.emit("vector", lambda e, E=E, i=i: e.tensor_scalar(
                    out=qc[:, i, :], in0=E[:, i, 0:128], scalar1=cwqk[:, 4 * i:4 * i + 1], scalar2=cbqk[:, i:i + 1],
                    op0=ALU.mult, op1=ALU.add), reads=[E, cwqk, cbqk], writes=[qc])
                for tap in (1, 2, 3):
                    P.emit("vector", lambda e, E=E, i=i, tap=tap: e.scalar_tensor_tensor(
                        out=qc[:, i, :], in0=E[:, i, tap:tap + 128], scalar=cwqk[:, 4 * i + tap:4 * i + tap + 1],
                        in1=qc[:, i, :], op0=ALU.mult, op1=ALU.add), reads=[E, cwqk, qc], writes=[qc])
            P.emit("scalar", lambda e: e.activation(out=qs[:, :, :], in_=qc[:, :, :], func=AF.Silu), reads=[qc], writes=[qs])
            P.emit("vector", lambda e: e.tensor_tensor(out=gt[:, :], in0=pgt[:, 0:4], in1=gb[:, :], op=ALU.add),
                   reads=[pgt, gb], writes=[gt])
            P.emit("scalar", lambda e: e.activation(out=e1[:, :], in_=gt[:, 2:4], func=AF.Exp, scale=-1.0), reads=[gt], writes=[e1])
            P.emit("vector", lambda e: e.tensor_scalar(out=e1[:, :], in0=e1[:, :], scalar1=1.0, scalar2=None, op0=ALU.add),
                   reads=[e1], writes=[e1])
            P.emit("scalar", lambda e: e.activation(out=sp[:, :], in_=e1[:, :], func=AF.Ln), reads=[e1], writes=[sp])
            P.mm(pgt, pgt[:, 4:6], triu, triu[:, :], sp, sp[:, :], start=True, stop=True)
            P.mm(pgt, pgt[:, 6:8], ones, ones[:, :], sp, sp[:, :], start=True, stop=True)
            P.emit("vector", lambda e: e.tensor_tensor(out=ws[:, :], in0=pgt[:, 4:6], in1=gt[:, 0:2], op=ALU.add),
                   reads=[pgt, gt], writes=[ws])
            P.emit("scalar", lambda e: e.activation(out=ws[:, :], in_=ws[:, :], func=AF.Exp), reads=[ws], writes=[ws])
            P.emit("scalar", lambda e: e.activation(out=et[:, :], in_=pgt[:, 4:6], func=AF.Exp, scale=-1.0), reads=[pgt], writes=[et])
            P.emit("scalar", lambda e: e.activation(out=dect[:, :], in_=pgt[:, 6:8], func=AF.Exp, scale=-1.0), reads=[pgt], writes=[dect])
            P.emit("vector", lambda e: e.scalar_tensor_tensor(out=kd[:, :], in0=ws[:, :], scalar=0.125, in1=dect[:, :],
                                                              op0=ALU.mult, op1=ALU.mult), reads=[ws, dect], writes=[kd])
            P.emit("vector", lambda e: e.tensor_copy(out=decc[0:64, :], in_=dect[0:64, 0:1]), reads=[dect], writes=[decc])
            P.emit("vector", lambda e: e.tensor_copy(out=decc[64:128, :], in_=dect[64:128, 1:2]), reads=[dect], writes=[decc])
            P.emit("scalar", lambda e: e.activation(out=vaug[:, :, 0:128], in_=pvo[:, 0:256].rearrange("p (h d) -> p h d", h=2),
                                                    func=AF.Copy), reads=[pvo], writes=[vaug])
            P.emit("scalar", lambda e: e.activation(out=sig[:, :], in_=pvo[:, 256:512], func=AF.Sigmoid), reads=[pvo], writes=[sig])
            O = otile.next()
            for h in range(2):
                P.emit("gpsimd", lambda e, h=h: e.tensor_scalar(out=qm[:, h, :], in0=qs[:, 0, :], scalar1=hmk[:, h:h + 1], scalar2=None,
                                                              op0=ALU.mult), reads=[qs, hmk], writes=[qm])
                P.emit("vector", lambda e, h=h: e.tensor_scalar(out=km[:, h, :], in0=qs[:, 1, :], scalar1=hmk[:, h:h + 1], scalar2=None,
                                                              op0=ALU.mult), reads=[qs, hmk], writes=[km])
            for h in range(2):
                hp = slice(h * 64, (h + 1) * 64)
                P.mm(psc, psc[:, :], km, km[:, h, :], qs, qs[:, 0, :], start=True, stop=True)
                S_ = sT.next()
                P.emit("vector", lambda e, S_=S_, h=h: e.scalar_tensor_tensor(
                    out=S_[:, :], in0=psc[:, :], scalar=ws[:, h:h + 1], in1=mask8[:, :], op0=ALU.mult, op1=ALU.mult),
                    reads=[psc, ws, mask8], writes=[S_])
                P.emit("tensor", lambda e, h=h: e.transpose(pkt[:, :], km[:, h, :], identb[:, :]),
                       reads=[km, identb], writes=[pkt])
                K_ = kdb.next()
                P.emit("scalar", lambda e, K_=K_, h=h: e.activation(out=K_[:, :], in_=pkt[:, :], func=AF.Copy, scale=kd[:, h:h + 1]),
                       reads=[pkt, kd], writes=[K_])
                P.mm(pnd, pnd[:, :], S_, S_[:, :], vaug, vaug[:, h, :], start=True, stop=False)
                P.mm(pnd, pnd[:, :], qm, qm[:, h, :], Cb, Cb[:, :], start=False, stop=True)
                R_ = res.next(); m_ = sm.next()
                P.emit("scalar", lambda e, R_=R_, h=h: e.activation(out=R_[:, :], in_=pnd[:, :], func=AF.Copy, scale=et[:, h:h + 1]),
                       reads=[pnd, et], writes=[R_])
                P.emit("scalar", lambda e, R_=R_, m_=m_: e.activation(out=m_[:, 0:1], in_=R_[:, 128:129], func=AF.Abs),
                       reads=[R_], writes=[m_])
                P.emit("vector", lambda e, m_=m_: e.tensor_scalar(out=m_[:, 0:1], in0=m_[:, 0:1], scalar1=1.0, scalar2=None,
                                                                op0=ALU.max), reads=[m_], writes=[m_])
                P.emit("vector", lambda e, m_=m_: e.reciprocal(out=m_[:, 0:1], in_=m_[:, 0:1]), reads=[m_], writes=[m_])
                P.emit("scalar", lambda e, R_=R_, m_=m_: e.activation(out=junk[:, :], in_=R_[:, 0:128], func=AF.Square, accum_out=m_[:, 1:2]),
                       reads=[R_], writes=[junk, m_])
                P.emit("vector", lambda e, m_=m_: e.tensor_tensor(out=m_[:, 2:3], in0=m_[:, 0:1], in1=m_[:, 0:1], op=ALU.mult),
                       reads=[m_], writes=[m_])
                P.emit("vector", lambda e, m_=m_: e.tensor_tensor(out=m_[:, 2:3], in0=m_[:, 2:3], in1=m_[:, 1:2], op=ALU.mult),
                       reads=[m_], writes=[m_])
                P.emit("scalar", lambda e, m_=m_: e.activation(out=m_[:, 3:4], in_=m_[:, 2:3], func=AF.Sqrt, scale=1.0 / 128, bias=pre.epsc[:, :]),
                       reads=[m_, pre.epsc], writes=[m_])
                P.emit("vector", lambda e, m_=m_: e.reciprocal(out=m_[:, 3:4], in_=m_[:, 3:4]), reads=[m_], writes=[m_])
                P.emit("vector", lambda e, m_=m_: e.tensor_tensor(out=m_[:, 4:5], in0=m_[:, 3:4], in1=m_[:, 0:1], op=ALU.mult),
                       reads=[m_], writes=[m_])
                P.emit("vector", lambda e, R_=R_, m_=m_, h=h: e.scalar_tensor_tensor(
                    out=ytmp[:, :], in0=R_[:, 0:128], scalar=m_[:, 4:5], in1=gn[:, h * 128:(h + 1) * 128], op0=ALU.mult, op1=ALU.mult),
                    reads=[R_, m_, gn], writes=[ytmp])
                P.emit("vector", lambda e, O=O, h=h: e.tensor_tensor(out=O[:, h * 128:(h + 1) * 128], in0=ytmp[:, :],
                                                                    in1=sig[:, h * 128:(h + 1) * 128], op=ALU.mult),
                       reads=[ytmp, sig], writes=[O])
                P.mm(pC, pC[:, :], K_, K_[:, :], vaug, vaug[:, h, :], start=(h == 0), stop=(h == 1))
            P.emit("vector", lambda e: e.scalar_tensor_tensor(out=Cf[:, :], in0=Cf[:, :], scalar=decc[:, 0:1], in1=pC[:, :],
                                                              op0=ALU.mult, op1=ALU.add), reads=[Cf, decc, pC], writes=[Cf])
            P.emit("gpsimd", lambda e: e.tensor_copy(out=Cb[:, :], in_=Cf[:, :]), reads=[Cf], writes=[Cb])
            P.dma(oout[c * 128:(c + 1) * 128, :], O[:, :], reads=[O], final=True)
        P.finish()
    return nc


def _mix_cols(inp, layer, m):
    sh_m, sc_m = m[0:D], m[D:2 * D]
    return np.ascontiguousarray(np.concatenate([_col(inp["norm_mix_g"][layer], 8), _col(sc_m, 8), _col(sh_m, 8)], axis=1))


def run_A_odd(x, layer, mod, inp):
    Bn, T, _ = x.shape
    nc = _get_nc(("Ao", T), lambda: build_A_odd(T))
    j = layer // 2
    w = inp["odd_w_in"][j]
    cw = inp["odd_conv_w"][j]
    cbv = inp["odd_conv_b"][j]
    gbv = inp["odd_gate_b"][j]
    gnv = inp["odd_norm_g"][j]
    in_maps = []
    for core in range(8):
        b, g = core // 4, core % 4
        q0, k0 = 2 * g * 64, 512 + 2 * g * 64
        v0, o0 = 1024 + 2 * g * 128, 2048 + 2 * g * 128
        wc = np.concatenate([w[:, q0:q0 + 128], w[:, k0:k0 + 128], w[:, v0:v0 + 256], w[:, o0:o0 + 256],
                             w[:, 3072 + 2 * g:3072 + 2 * g + 2], w[:, 3080 + 2 * g:3080 + 2 * g + 2]], axis=1)
        cwqk = np.concatenate([cw[:, q0:q0 + 128].T, cw[:, k0:k0 + 128].T], axis=1)
        cbqk = np.stack([cbv[q0:q0 + 128], cbv[k0:k0 + 128]], axis=1)
        gb4 = np.concatenate([gbv[2 * g:2 * g + 2], gbv[8 + 2 * g:8 + 2 * g + 2]])
        in_maps.append({
            "xin": np.ascontiguousarray(x[b]),
            "cols": _mix_cols(inp, layer, mod[b]),
            "w": np.ascontiguousarray(wc, np.float32),
            "cwqk": np.ascontiguousarray(cwqk, np.float32),
            "cbqk": np.ascontiguousarray(cbqk, np.float32),
            "gb_rep": _rep(gb4),
            "gn_rep": _rep(gnv[2 * g * 128:(2 * g + 2) * 128]),
        })
    res = run_bass_kernel_spmd(nc, in_maps, core_ids=list(range(8)))
    o = np.empty((Bn, T, D), np.float32)
    for core in range(8):
        b, g = core // 4, core % 4
        o[b, :, 2 * g * 128:(2 * g + 2) * 128] = res.results[core]["oout"]
    return o


def build_M():
    nc = bass.Bass("TRN2", target_bir_lowering=False)
    di = lambda n, s: nc.dram_tensor(n, s, F32, kind="ExternalInput").ap()
    cT_d = di("cT", [128, 16])
    w_d = di("w", [D, 3072])
    b_d = di("b2", [2, 3072])
    mout = nc.dram_tensor("mout", [2, 3072], F32, kind="ExternalOutput").ap()
    with ExitStack() as st:
        P = Prog(nc, st)
        cT = P.sb([128, 16], F32, "cT_sb")
        cs = P.sb([128, 16], F32, "cs_sb")
        Wm = [P.sb([128, 3072], F32, f"Wm{k}") for k in range(8)]
        bb = P.sb([2, 3072], F32, "bb")
        mo = P.sb([2, 3072], F32, "mo")
        pm = RR([P.ps([128, 512], F32, f"pm{i}") for i in range(2)])
        P.dma(cT[:, :], cT_d, writes=[cT])
        P.dma(bb[:, :], b_d, writes=[bb])
        for k in range(8):
            P.dma(Wm[k][:, :], w_d[k * 128:(k + 1) * 128, :], writes=[Wm[k]])
        P.emit("scalar", lambda e: e.activation(out=cs[:, :], in_=cT[:, :], func=AF.Silu), reads=[cT], writes=[cs])
        for n in range(6):
            p_ = pm.next()
            for k in range(8):
                P.mm(p_, p_[0:2, :], cs, cs[:, 2 * k:2 * k + 2], Wm[k], Wm[k][:, n * 512:(n + 1) * 512], start=(k == 0), stop=(k == 7))
            P.emit("vector", lambda e, p_=p_, n=n: e.tensor_tensor(out=mo[:, n * 512:(n + 1) * 512], in0=p_[0:2, :],
                                                                  in1=bb[:, n * 512:(n + 1) * 512], op=ALU.add),
                   reads=[p_, bb], writes=[mo])
        P.dma(mout, mo[:, :], reads=[mo], final=True)
        P.finish()
    return nc


def run_M(inp):
    nc = _get_nc(("M",), build_M)
    c = np.asarray(inp["c"], np.float32)
    cT = np.ascontiguousarray(c.reshape(2, 8, 128).transpose(2, 1, 0).reshape(128, 16))
    in_maps = []
    for core in range(8):
        l, hf = core // 2, core % 2
        in_maps.append({
            "cT": cT,
            "w": np.ascontiguousarray(inp["ada_w"][l][:, hf * 3072:(hf + 1) * 3072]),
            "b2": np.ascontiguousarray(np.broadcast_to(inp["ada_b"][l][None, hf * 3072:(hf + 1) * 3072], (2, 3072))),
        })
    res = run_bass_kernel_spmd(nc, in_maps, core_ids=list(range(8)))
    mod = np.empty((4, 2, 6 * D), np.float32)
    for core in range(8):
        l, hf = core // 2, core % 2
        mod[l, :, hf * 3072:(hf + 1) * 3072] = res.results[core]["mout"]
    return mod


NW_EVEN = 896


def build_A_even(T, phases=3, dbg=9, dbg2=9):
    NB = T // 128
    nc = bass.Bass("TRN2", target_bir_lowering=False)
    di = lambda n, s: nc.dram_tensor(n, s, F32, kind="ExternalInput").ap()
    xin = di("xin", [T, D])
    cols_d = di("cols", [128, 24])
    w_d = di("w", [D, NW_EVEN])
    lb_d = di("lbraw", [128, 4])
    gn_d = di("gn_rep", [128, 128])
    oout = nc.dram_tensor("oout", [T, 256], F32, kind="ExternalOutput").ap()
    with ExitStack() as st:
        P = Prog(nc, st)
        pre = Pre(P, cols_d)
        W = P.sb([128, 8, NW_EVEN], BF16, "W")
        for k in range(8):
            P.dma(W[:, k, :], w_d[k * 128:(k + 1) * 128, :], writes=[W], q="gpsimd")
        lbr = P.sb([128, 4], F32, "lbr"); P.dma(lbr[:, :], lb_d, writes=[lbr])
        gn = P.sb([128, 128], F32, "gn_sb"); P.dma(gn[:, :], gn_d, writes=[gn])
        lw = P.sb([128, 8], F32, "lw")
        V = lambda fn, r, w: P.emit("vector", fn, reads=r, writes=w)
        A = lambda fn, r, w: P.emit("scalar", fn, reads=r, writes=w)
        G = lambda fn, r, w: P.emit("gpsimd", fn, reads=r, writes=w)
        A(lambda e: e.activation(out=lw[:, 0:2], in_=lbr[:, 0:2], func=AF.Exp), [lbr], [lw])
        V(lambda e: e.tensor_tensor(out=lw[:, 2:3], in0=lw[:, 0:1], in1=lw[:, 1:2], op=ALU.add), [lw], [lw])
        V(lambda e: e.reciprocal(out=lw[:, 2:3], in_=lw[:, 2:3]), [lw], [lw])
        V(lambda e: e.tensor_scalar(out=lw[:, 3:5], in0=lw[:, 0:2], scalar1=lw[:, 2:3], scalar2=None, op0=ALU.mult), [lw], [lw])
        V(lambda e: e.tensor_tensor(out=lw[:, 7:8], in0=lw[:, 3:4], in1=lw[:, 4:5], op=ALU.add), [lw], [lw])
        V(lambda e: e.tensor_tensor(out=lw[:, 7:8], in0=lw[:, 7:8], in1=lw[:, 3:4], op=ALU.subtract), [lw], [lw])
        V(lambda e: e.tensor_tensor(out=lw[:, 5:6], in0=lw[:, 7:8], in1=lbr[:, 3:4], op=ALU.mult), [lw, lbr], [lw])
        V(lambda e: e.tensor_scalar(out=lw[:, 6:7], in0=lw[:, 5:6], scalar1=-1.0, scalar2=1.0, op0=ALU.mult, op1=ALU.add), [lw], [lw])
        ident = pre.ident
        onec = P.sb([128, 1], F32, "onec")
        G(lambda e: e.memset(onec[:, :], 1.0), [], [onec])
        rmask = P.sb([128, 128], F32, "rmask")
        G(lambda e: e.memset(rmask[:, :], 1.0), [], [rmask])
        for m in range(4):
            G(lambda e, m=m: e.memset(rmask[:, m * 32:m * 32 + 1], 0.0), [], [rmask])
        bdmask = _tri_mask(P, "bdmask", F32, 1.0, 0, -1, 1)
        for m in range(4):
            if m < 3:
                G(lambda e, m=m: e.memset(bdmask[m * 32:(m + 1) * 32, (m + 1) * 32:128], 0.0), [bdmask], [bdmask])
        maskst = _tri_mask(P, "maskst", F32, 1.0, -1, -1, 1)
        negincl = _tri_mask(P, "negincl", F32, -1.0, 0, 1, -1)
        ones1 = P.sb([128, 128], F32, "ones1")
        G(lambda e: e.memset(ones1[:, :], 0.0), [], [ones1])
        G(lambda e: e.memset(ones1[0:1, :], 1.0), [ones1], [ones1])
        qbT = P.sb([128, T], BF16, "qbT")
        kbT = P.sb([128, T], BF16, "kbT")
        vb = P.sb([128, NB, 128], BF16, "vb")
        sg = P.sb([128, 128], F32, "sg")
        kk = P.sb([128, 128], F32, "kk")
        lf = P.sb([128, 128], F32, "lf")
        bT = P.sb([128, 128], F32, "bT")
        eb = P.sb([128, 128], F32, "eb")
        enb = P.sb([128, 128], F32, "enb")
        qtT = RR([P.sb([128, 128], BF16, f"qtT{i}") for i in range(2)])
        ktT = RR([P.sb([128, 128], BF16, f"ktT{i}") for i in range(2)])
        khT = P.sb([128, 128], F32, "khT")
        dec4 = RR([P.sb([128, 4], F32, f"dec4{i}") for i in range(2)])
        vbf = RR([P.sb([128, 128], BF16, f"vbf{i}") for i in range(2)])
        vbm = RR([P.sb([128, 4, 128], BF16, f"vbm{i}") for i in range(2)])
        rm4 = P.sb([128, 4], F32, "rm4")
        G(lambda e: e.memset(rm4[:, :], 0.0), [], [rm4])
        for m in range(3):
            G(lambda e, m=m: e.memset(rm4[m * 32:(m + 1) * 32, m:m + 1], 1.0), [rm4], [rm4])
        G(lambda e: e.memset(rm4[64:128, 3:4], 1.0), [rm4], [rm4])
        G(lambda e: e.memset(rm4[64:96, 3:4], 0.0), [rm4], [rm4])
        qzs = [P.sb([128, 640], BF16, f"qz{i}") for i in range(2)]
        for t_ in qzs:
            G(lambda e, t_=t_: e.memset(t_[:, :], 0.0), [], [t_])
        qz = RR(qzs)
        scm = RR([P.sb([128, 128], BF16, f"scm{i}") for i in range(2)])
        khtok = RR([P.sb([128, 128], BF16, f"khtok{i}") for i in range(2)])
        Sf = P.sb([128, 128], F32, "Sf")
        Sb = RR([P.sb([128, 128], BF16, f"Sb{i}") for i in range(3)])
        G(lambda e: e.memset(Sf[:, :], 0.0), [], [Sf])
        S0 = Sb.next()
        G(lambda e: e.memset(S0[:, :], 0.0), [], [S0])
        junk = P.sb([128, 128], F32, "junk")
        hs = RR([P.sb([128, 2], F32, f"hs{i}") for i in range(2)])
        slg = P.sb([128, 128], F32, "slg")
        ytmp = P.sb([128, 128], F32, "ytmp")
        oa = RR([P.sb([128, 128], F32, f"oa{i}") for i in range(2)])
        banks = [P.ps([128, 512], F32, f"bank{i}") for i in range(8)]
        pT = [P.view(banks[i][:, :].rearrange("p (a b) -> p a b", a=4)) for i in range(2)]
        pF = P.view(banks[2][:, :].rearrange("p (a b) -> p a b", a=4))
        pTk = P.view(banks[3][:, 0:384])
        psc = P.view(banks[4][:, 0:128])
        pkt = P.view(banks[4][:, 128:256])
        po = P.view(banks[5][:, 0:128])
        pSt = P.view(banks[6][:, 0:128])
        Scur = S0
        for c in range(NB):
            if dbg < -1:
                continue
            H = pre.run(xin[c * 128:(c + 1) * 128, :], pT)
            if dbg < 0:
                continue
            for i in range(4):
                for k in range(8):
                    P.mm(pF, pF[:, i, :], W, W[:, k, i * 128:(i + 1) * 128], H, H[:, k, :], start=(k == 0), stop=(k == 7))
            for k in range(8):
                P.mm(pTk, pTk[:, :], H, H[:, k, :], W, W[:, k, 512:896], start=(k == 0), stop=(k == 7))
            cs = slice(c * 128, (c + 1) * 128)
            V(lambda e, cs=cs: e.tensor_scalar(out=qbT[:, cs], in0=pF[:, 2, :], scalar1=128.0 ** -0.5, scalar2=None, op0=ALU.mult),
              [pF], [qbT])
            V(lambda e, cs=cs: e.tensor_copy(out=kbT[:, cs], in_=pF[:, 3, :]), [pF], [kbT])
            V(lambda e, c=c: e.tensor_copy(out=vb[:, c, :], in_=pTk[:, 256:384]), [pTk], [vb])
            if dbg < 1:
                continue
            A(lambda e: e.activation(out=sg[:, :], in_=pF[:, 1, :], func=AF.Sigmoid), [pF], [sg])
            V(lambda e: e.tensor_scalar(out=sg[:, :], in0=sg[:, :], scalar1=lw[:, 6:7], scalar2=lw[:, 5:6], op0=ALU.mult, op1=ALU.add),
              [sg, lw], [sg])
            G(lambda e: e.tensor_scalar(out=kk[:, :], in0=sg[:, :], scalar1=-1.0, scalar2=1.0, op0=ALU.mult, op1=ALU.add), [sg], [kk])
            A(lambda e: e.activation(out=lf[:, :], in_=sg[:, :], func=AF.Ln), [sg], [lf])
            V(lambda e: e.tensor_tensor_scan(out=bT[:, :], data0=rmask[:, :], data1=lf[:, :], initial=0.0, op0=ALU.mult, op1=ALU.add),
              [rmask, lf], [bT])
            A(lambda e: e.activation(out=eb[:, :], in_=bT[:, :], func=AF.Exp), [bT], [eb])
            A(lambda e: e.activation(out=enb[:, :], in_=bT[:, :], func=AF.Exp, scale=-1.0), [bT], [enb])
            d4 = dec4.next()
            A(lambda e, d4=d4: e.activation(out=d4[:, :], in_=bT[:, 31:128:32], func=AF.Exp), [bT], [d4])
            Q_, K_ = qtT.next(), ktT.next()
            V(lambda e, Q_=Q_: e.tensor_tensor(out=Q_[:, :], in0=pF[:, 0, :], in1=eb[:, :], op=ALU.mult), [pF, eb], [Q_])
            V(lambda e, K_=K_: e.tensor_tensor(out=K_[:, :], in0=kk[:, :], in1=enb[:, :], op=ALU.mult), [kk, enb], [K_])
            for m in range(4):
                G(lambda e, m=m, K_=K_, d4=d4: e.tensor_scalar(out=khT[:, m * 32:(m + 1) * 32], in0=K_[:, m * 32:(m + 1) * 32],
                                                              scalar1=d4[:, m:m + 1], scalar2=None, op0=ALU.mult),
                  [K_, d4], [khT])
            if dbg < 2:
                continue
            Vb = vbf.next()
            A(lambda e, Vb=Vb: e.activation(out=Vb[:, :], in_=pTk[:, 0:128], func=AF.Copy), [pTk], [Vb])
            P.mm(psc, psc[:, :], K_, K_[:, :], Q_, Q_[:, :], start=True, stop=True)
            Sm = scm.next()
            V(lambda e, Sm=Sm: e.tensor_tensor(out=Sm[:, :], in0=psc[:, :], in1=bdmask[:, :], op=ALU.mult), [psc, bdmask], [Sm])
            P.tr(pkt, pkt[:, :], khT, khT[:, :], ident, ident[:, :])
            Kt = khtok.next()
            A(lambda e, Kt=Kt: e.activation(out=Kt[:, :], in_=pkt[:, :], func=AF.Copy), [pkt], [Kt])
            if dbg < 3:
                continue
            Vm = vbm.next()
            for m in range(4):
                if m < 2:
                    A(lambda e, Vm=Vm, m=m: e.activation(out=Vm[:, m, :], in_=pTk[:, 0:128], func=AF.Copy, scale=rm4[:, m:m + 1]),
                      [pTk, rm4], [Vm])
                else:
                    V(lambda e, Vm=Vm, m=m: e.tensor_scalar(out=Vm[:, m, :], in0=pTk[:, 0:128], scalar1=rm4[:, m:m + 1], scalar2=None,
                                                            op0=ALU.mult), [pTk, rm4], [Vm])
            Z_ = qz.next()
            G(lambda e, Z_=Z_, Q_=Q_: e.tensor_copy(out=Z_[:, 0:640].rearrange("p (m x) -> p m x", x=160)[:, :, 0:32],
                                                    in_=Q_[:, :].rearrange("p (m t) -> p m t", t=32)), [Q_], [Z_])
            if dbg < 4:
                continue
            P.mm(po, po[:, :], Sm, Sm[:, :], Vb, Vb[:, :], start=True, stop=False)
            for m in range(4):
                ms = slice(m * 32, (m + 1) * 32)
                P.mm(po, po[:, :], Z_, Z_[:, m * 128:(m + 1) * 128], Scur, Scur[:, :], start=False, stop=(m == 3))
                P.mm(pSt, pSt[:, :], Kt, Kt[:, :], Vm, Vm[:, m, :], start=True, stop=True)
                V(lambda e, m=m, d4=d4: e.scalar_tensor_tensor(out=Sf[:, :], in0=Sf[:, :], scalar=d4[:, m:m + 1], in1=pSt[:, :],
                                                              op0=ALU.mult, op1=ALU.add), [Sf, d4, pSt], [Sf])
                Scur = Sb.next()
                A(lambda e, Scur=Scur: e.activation(out=Scur[:, :], in_=Sf[:, :], func=AF.Copy), [Sf], [Scur])
            if dbg < 5:
                continue
            h_ = hs.next()
            A(lambda e, h_=h_: e.activation(out=junk[:, :], in_=po[:, :], func=AF.Square, accum_out=h_[:, 0:1]), [po], [junk, h_])
            A(lambda e, h_=h_: e.activation(out=h_[:, 1:2], in_=h_[:, 0:1], func=AF.Sqrt, scale=1.0 / 128, bias=pre.epsc[:, :]),
              [h_, pre.epsc], [h_])
            V(lambda e, h_=h_: e.reciprocal(out=h_[:, 1:2], in_=h_[:, 1:2]), [h_], [h_])
            A(lambda e: e.activation(out=slg[:, :], in_=pTk[:, 128:256], func=AF.Silu), [pTk], [slg])
            V(lambda e, h_=h_: e.scalar_tensor_tensor(out=ytmp[:, :], in0=po[:, :], scalar=h_[:, 1:2], in1=gn[:, :],
                                                      op0=ALU.mult, op1=ALU.mult), [po, h_, gn], [ytmp])
            O = oa.next()
            V(lambda e, O=O: e.tensor_tensor(out=O[:, :], in0=ytmp[:, :], in1=slg[:, :], op=ALU.mult), [ytmp, slg], [O])
            P.dma(oout[c * 128:(c + 1) * 128, 0:128], O[:, :], reads=[O], final=True)

        P.barrier()
        pz = RR([P.view(banks[i][:, :]) for i in range(2)])
        pr = RR([P.view(banks[2 + i][:, :]) for i in range(2)])
        poq = [P.view(banks[4 + q][:, 0:128]) for q in range(4)]
        u_ = RR([P.sb([128, 512], F32, f"u{i}") for i in range(2)])
        l_ = RR([P.sb([128, 512], F32, f"l{i}") for i in range(2)])
        er_ = RR([P.sb([128, 512], F32, f"er{i}") for i in range(2)])
        aT_ = RR([P.sb([128, 512], BF16, f"aT{i}") for i in range(2)])
        accrow = P.sb([128, 512], F32, "accrow")
        G(lambda e: e.memset(accrow[:, :], 0.0), [], [accrow])
        osb = RR([P.sb([128, 4, 128], F32, f"osb{i}") for i in range(2)])
        lh_ = RR([P.sb([128, 512], BF16, f"lh{i}") for i in range(2)])
        ll_ = RR([P.sb([128, 512], BF16, f"ll{i}") for i in range(2)])
        acch = P.sb([128, 512], BF16, "acch")
        accl = P.sb([128, 512], BF16, "accl")
        G(lambda e: e.memset(acch[:, :], 0.0), [], [acch])
        G(lambda e: e.memset(accl[:, :], 0.0), [], [accl])
        neginclb = P.sb([128, 128], BF16, "neginclb")
        ones1b = P.sb([128, 128], BF16, "ones1b")
        G(lambda e: e.tensor_copy(out=neginclb[:, :], in_=negincl[:, :]), [negincl], [neginclb])
        G(lambda e: e.tensor_copy(out=ones1b[:, :], in_=ones1[:, :]), [ones1], [ones1b])
        for g in range(NB // 4 if phases & 2 else 0):
            i0 = 4 * g
            G(lambda e: e.memset(accrow[0:1, :], 0.0), [], [accrow])
            G(lambda e: e.memset(acch[0:1, :], 0.0), [], [acch])
            G(lambda e: e.memset(accl[0:1, :], 0.0), [], [accl])
            for j in range(i0 + 3, -1, -1):
                q0 = max(0, j - i0)
                c0 = q0 * 128
                diag = j >= i0
                z = pz.next(); r = pr.next(); u = u_.next(); l = l_.next(); er = er_.next(); aT = aT_.next()
                P.mm(z, z[:, c0:512], kbT, kbT[:, j * 128:(j + 1) * 128], qbT, qbT[:, i0 * 128 + c0:(i0 + 4) * 128],
                     start=True, stop=True)
                A(lambda e, z=z, u=u, c0=c0: e.activation(out=u[:, c0:512], in_=z[:, c0:512], func=AF.Exp), [z], [u])
                A(lambda e, l=l, u=u, c0=c0: e.activation(out=l[:, c0:512], in_=u[:, c0:512], func=AF.Ln, bias=onec[:, :]),
                  [u, onec], [l])
                if diag:
                    V(lambda e, l=l, c0=c0: e.tensor_tensor(out=l[:, c0:c0 + 128], in0=l[:, c0:c0 + 128], in1=maskst[:, :], op=ALU.mult),
                      [l, maskst], [l])
                if dbg2 < 1:
                    continue
                lh = lh_.next(); ll = ll_.next()
                G(lambda e, l=l, lh=lh, c0=c0: e.tensor_copy(out=lh[:, c0:512], in_=l[:, c0:512]), [l], [lh])
                V(lambda e, l=l, lh=lh, ll=ll, c0=c0: e.tensor_tensor(out=ll[:, c0:512], in0=l[:, c0:512], in1=lh[:, c0:512],
                                                                   op=ALU.subtract), [l, lh], [ll])
                P.mm(r, r[:, c0:512], neginclb, neginclb[:, :], lh, lh[:, c0:512], start=True, stop=False)
                P.mm(r, r[:, c0:512], neginclb, neginclb[:, :], ll, ll[:, c0:512], start=False, stop=False)
                P.mm(r, r[:, c0:512], ones1b, ones1b[:, :], acch, acch[:, c0:512], start=False, stop=False)
                P.mm(r, r[:, c0:512], ones1b, ones1b[:, :], accl, accl[:, c0:512], start=False, stop=True)
                A(lambda e, r=r, er=er, c0=c0: e.activation(out=er[:, c0:512], in_=r[:, c0:512], func=AF.Exp), [r], [er])
                A(lambda e, r=r, c0=c0: e.activation(out=accrow[0:1, c0:512], in_=r[0:1, c0:512], func=AF.Identity), [r], [accrow])
                V(lambda e, c0=c0: e.tensor_copy(out=acch[0:1, c0:512], in_=accrow[0:1, c0:512]), [accrow], [acch])
                V(lambda e, c0=c0: e.tensor_tensor(out=accl[0:1, c0:512], in0=accrow[0:1, c0:512], in1=acch[0:1, c0:512],
                                                   op=ALU.subtract), [accrow, acch], [accl])
                if dbg2 < 2:
                    continue
                V(lambda e, aT=aT, u=u, er=er, c0=c0: e.tensor_tensor(out=aT[:, c0:512], in0=u[:, c0:512], in1=er[:, c0:512], op=ALU.mult),
                  [u, er], [aT])
                if diag:
                    V(lambda e, aT=aT, c0=c0: e.tensor_tensor(out=aT[:, c0:c0 + 128], in0=aT[:, c0:c0 + 128], in1=maskst[:, :], op=ALU.mult),
                      [aT, maskst], [aT])
                if dbg2 < 3:
                    continue
                for q in range(q0, 4):
                    P.mm(poq[q], poq[q][:, :], aT, aT[:, q * 128:(q + 1) * 128], vb, vb[:, j, :],
                         start=(j == i0 + q), stop=(j == 0))
            if dbg2 < 3:
                continue
            ob = osb.next()
            for q in range(4):
                A(lambda e, ob=ob, q=q: e.activation(out=ob[:, q, :], in_=poq[q][:, :], func=AF.Copy), [poq[q]], [ob])
            P.dma(oout[i0 * 128:(i0 + 4) * 128, 128:256].rearrange("(q p) d -> p q d", p=128), ob[:, :, :], reads=[ob], final=True)
        P.finish()
    return nc


def run_A_even(x, layer, mod, inp):
    Bn, T, _ = x.shape
    nc = _get_nc(("Ae", T), lambda: build_A_even(T))
    j = layer // 2
    w = inp["even_w_in"][j]
    lbraw = inp["hgrn_lb"]
    gnv = inp["hgrn_norm_g"][j]
    in_maps = []
    for core in range(8):
        b, g = core // 4, core % 4
        hs_ = slice(g * 128, (g + 1) * 128)
        blk = lambda n: w[:, n * 512 + g * 128:n * 512 + (g + 1) * 128]
        wc = np.concatenate([blk(0), blk(1), blk(4), blk(5), blk(2), blk(3), blk(6)], axis=1)
        sel = np.zeros((128, 2), np.float32); sel[:, j] = 1.0
        lb4 = np.concatenate([lbraw[0, hs_][:, None], lbraw[1, hs_][:, None], sel], axis=1)
        in_maps.append({
            "xin": np.ascontiguousarray(x[b]),
            "cols": _mix_cols(inp, layer, mod[b]),
            "w": np.ascontiguousarray(wc, np.float32),
            "lbraw": np.ascontiguousarray(lb4, np.float32),
            "gn_rep": _rep(gnv[hs_]),
        })
    res = run_bass_kernel_spmd(nc, in_maps, core_ids=list(range(8)))
    o = np.empty((Bn, T, D), np.float32)
    for core in range(8):
        b, g = core // 4, core % 4
        r_ = res.results[core]["oout"]
        o[b, :, g * 128:(g + 1) * 128] = r_[:, 0:128]
        o[b, :, 512 + g * 128:512 + (g + 1) * 128] = r_[:, 128:256]
    return o


def kernel(**inputs):
    inp = {k: np.ascontiguousarray(np.asarray(v, dtype=np.float32)) for k, v in inputs.items()}
    x = inp["x"]
    mod = run_M(inp)
    for layer in range(4):
        if layer % 2 == 0:
            o = run_A_even(x, layer, mod[layer], inp)
        else:
            o = run_A_odd(x, layer, mod[layer], inp)
        x = run_B(x, o, layer, mod[layer], inp, final=(layer == 3))
    return x
```
